# Optimizing a Trainium2 kernel written in Bass

```python
import jax, jax.numpy as jnp
from jax import lax
import numpy as np

D_MODEL = 1024
BATCH = 8
SEQ = 2048
DEPTH = 1

MEM_LEN = 256
HEAD_DIM = 64
POOL_WINDOWS = (2, 4, 8, 16)
POOL_GROUPS = len(POOL_WINDOWS)
POOL_WIDTH = D_MODEL // 4
POOL_CH = POOL_WIDTH // POOL_GROUPS
SB_WIDTH = D_MODEL // 2
SB_HEADS = SB_WIDTH // HEAD_DIM
MEM_HEADS = 4
MEM_WIDTH = D_MODEL // 4
MEM_HEAD_DIM = MEM_WIDTH // MEM_HEADS
MIX_WIDTH = POOL_WIDTH + SB_WIDTH + MEM_WIDTH
IN_SPLITS = (POOL_WIDTH, POOL_WIDTH, SB_WIDTH, SB_WIDTH, SB_WIDTH, SB_WIDTH, MEM_WIDTH, MEM_WIDTH)
IN_WIDTH = sum(IN_SPLITS)
Q_BLOCK = 128
EPS = 1e-6

kernel_name = "hybrid_pool_stickbreak_memory_layer"


def rmsnorm(x, g):
    xf = x.astype(jnp.float32)
    y = xf * lax.rsqrt(jnp.mean(xf * xf, axis=-1, keepdims=True) + EPS)
    return (y * g.astype(jnp.float32)).astype(x.dtype)


def pool_mixer(u, pool_w, pool_scale):
    B, S, _ = u.shape
    uf = u.astype(jnp.float32).reshape(B, S, POOL_GROUPS, POOL_CH)
    csum = jnp.concatenate([jnp.zeros_like(uf[:, :1]), jnp.cumsum(uf, axis=1)], axis=1)
    t = jnp.arange(S)
    means = []
    for g, w in enumerate(POOL_WINDOWS):
        lo = jnp.maximum(t + 1 - w, 0)
        win_sum = csum[:, 1:, g] - csum[:, lo, g]
        count = (t + 1 - lo).astype(jnp.float32)
        means.append(win_sum / count[None, :, None])
    pooled = (jnp.stack(means, axis=2) - uf).astype(u.dtype)
    y = jnp.einsum('bsgc,gcd->bsgd', pooled, pool_w)
    return y.reshape(B, S, POOL_WIDTH) * pool_scale


def stick_breaking_attention(q, k, v):
    B, S, H, Dh = q.shape
    scale = Dh ** -0.5
    outs = []
    for i in range(S // Q_BLOCK):
        q0 = i * Q_BLOCK
        kv_len = q0 + Q_BLOCK
        qb = q[:, q0:kv_len]
        kb = k[:, :kv_len]
        vb = v[:, :kv_len]
        z = jnp.einsum('bqhd,bkhd->bhqk', qb, kb).astype(jnp.float32) * scale
        t_idx = q0 + jnp.arange(Q_BLOCK)
        s_idx = jnp.arange(kv_len)
        mask = s_idx[None, :] < t_idx[:, None]
        log_beta = jax.nn.log_sigmoid(z)
        log_1m_beta = jnp.where(mask, jax.nn.log_sigmoid(-z), 0.0)
        cum = jnp.cumsum(log_1m_beta, axis=-1)
        log_a = log_beta + cum[..., -1:] - cum
        a = jnp.where(mask, jnp.exp(log_a), 0.0)
        outs.append(jnp.einsum('bhqk,bkhd->bqhd', a.astype(v.dtype), vb))
    return jnp.concatenate(outs, axis=1)


def memory_attention(q, mem_k, mem_v, q_norm_g, k_norm_g):
    q = rmsnorm(q, q_norm_g)
    mem_k = rmsnorm(mem_k, k_norm_g)
    s = jnp.einsum('bqhd,bmhd->bhqm', q, mem_k).astype(jnp.float32) * (q.shape[-1] ** -0.5)
    p = jax.nn.softmax(s, axis=-1)
    return jnp.einsum('bhqm,bmhd->bqhd', p.astype(mem_v.dtype), mem_v)


def setup_inputs(seed: int = 0) -> dict:
    key = jax.random.key(seed)
    ks = jax.random.split(key, 12)
    f32 = jnp.float32
    x = jax.random.normal(ks[0], (BATCH, SEQ, D_MODEL), f32)
    mem = jax.random.normal(ks[1], (BATCH, MEM_LEN, D_MODEL), f32)
    norm_g = 1.0 + 0.02 * jax.random.normal(ks[2], (DEPTH, D_MODEL), f32)
    w_in = jax.random.normal(ks[3], (DEPTH, D_MODEL, IN_WIDTH), f32) * D_MODEL ** -0.5
    pool_w = jax.random.normal(ks[4], (DEPTH, POOL_GROUPS, POOL_CH, POOL_CH), f32) * POOL_CH ** -0.5
    pool_scale = 1.0 + 0.1 * jax.random.normal(ks[5], (DEPTH, POOL_WIDTH), f32)
    mem_norm_g = 1.0 + 0.02 * jax.random.normal(ks[6], (DEPTH, D_MODEL), f32)
    w_mem_kv = jax.random.normal(ks[7], (DEPTH, D_MODEL, 2 * MEM_WIDTH), f32) * D_MODEL ** -0.5
    q_norm_g = 1.0 + 0.02 * jax.random.normal(ks[8], (DEPTH, MEM_HEAD_DIM), f32)
    k_norm_g = 1.0 + 0.02 * jax.random.normal(ks[9], (DEPTH, MEM_HEAD_DIM), f32)
    w_out = jax.random.normal(ks[10], (DEPTH, MIX_WIDTH, D_MODEL), f32) * MIX_WIDTH ** -0.5
    return {"x": x, "mem": mem, "norm_g": norm_g, "w_in": w_in, "pool_w": pool_w,
            "pool_scale": pool_scale, "mem_norm_g": mem_norm_g, "w_mem_kv": w_mem_kv,
            "q_norm_g": q_norm_g, "k_norm_g": k_norm_g, "w_out": w_out}


def reference(x, mem, norm_g, w_in, pool_w, pool_scale, mem_norm_g, w_mem_kv, q_norm_g, k_norm_g, w_out):
    B, S, _ = x.shape
    M = mem.shape[1]
    split_points = list(np.cumsum(IN_SPLITS)[:-1])
    for l in range(DEPTH):
        h = rmsnorm(x, norm_g[l])
        proj = jnp.einsum('bsd,de->bse', h, w_in[l])
        (pool_v, pool_gate, sb_q, sb_k, sb_v, sb_gate,
         mem_q, mem_gate) = jnp.split(proj, split_points, axis=-1)

        y_pool = pool_mixer(pool_v, pool_w[l], pool_scale[l]) * jax.nn.silu(pool_gate)

        heads = lambda t: t.reshape(B, S, SB_HEADS, HEAD_DIM)
        y_sb = stick_breaking_attention(heads(sb_q), heads(sb_k), heads(sb_v)).reshape(B, S, SB_WIDTH)
        y_sb = y_sb * jax.nn.silu(sb_gate)

        mkv = jnp.einsum('bmd,de->bme', rmsnorm(mem, mem_norm_g[l]), w_mem_kv[l])
        mem_k, mem_v = jnp.split(mkv, 2, axis=-1)
        mem_k = mem_k.reshape(B, M, MEM_HEADS, MEM_HEAD_DIM)
        mem_v = mem_v.reshape(B, M, MEM_HEADS, MEM_HEAD_DIM)
        y_mem = memory_attention(mem_q.reshape(B, S, MEM_HEADS, MEM_HEAD_DIM), mem_k, mem_v,
                                 q_norm_g[l], k_norm_g[l]).reshape(B, S, MEM_WIDTH)
        y_mem = y_mem * jax.nn.silu(mem_gate)

        mixed = jnp.concatenate([y_pool, y_sb, y_mem], axis=-1)
        x = x + jnp.einsum('bse,ed->bsd', mixed, w_out[l])
    return x
```

```python
import numpy as np
import concourse.bass as bass
import concourse.mybir as mybir
from concourse.bass_utils import run_bass_kernel_spmd

F32 = mybir.dt.float32
F32R = mybir.dt.float32r
BF16 = mybir.dt.bfloat16
AF = mybir.ActivationFunctionType
ALU = mybir.AluOpType
AX = mybir.AxisListType

S_LEN = 2048
D = 1024
NT = 4
EPS = 1e-6


class Res:
    __slots__ = ("name", "w", "rs")

    def __init__(self, name):
        self.name = name
        self.w = None
        self.rs = []


class DmaSem:
    def __init__(self, nc, name):
        self.h = nc.alloc_semaphore(name)
        self.v = 0


class Sched:
    def __init__(self, nc):
        self.nc = nc
        self.eng = {"pe": nc.tensor, "act": nc.scalar, "dve": nc.vector,
                    "pool": nc.gpsimd, "sp": nc.sync}
        self.sem = {k: nc.alloc_semaphore("prog_" + k) for k in self.eng}
        self.cnt = {k: 0 for k in self.eng}
        self.waited = {}
        self.rec = None

    def _deps(self, e, reads, writes, attach_ok=False):
        need = {}
        selfv = 0

        def add(tok, kind):
            nonlocal selfv
            if tok is None:
                return
            h, v, pe = tok
            if pe == e:
                if e != "pe" and kind != "waw":
                    selfv = max(selfv, v)
                return
            key = id(h)
            if need.get(key, (None, 0))[1] < v:
                need[key] = (h, v)
        for r in reads:
            add(r.w, "raw")
        for r in writes:
            add(r.w, "waw")
            for t in r.rs:
                add(t, "war")
        todo = []
        for key, (h, v) in need.items():
            wk = (e, key)
            if self.waited.get(wk, 0) >= v:
                continue
            self.waited[wk] = v
            todo.append((h, v))
        attach = None
        if selfv > self.waited.get((e, "self"), 0):
            self.waited[(e, "self")] = selfv
            attach = (self.sem[e], selfv)
        elif attach_ok and todo:
            attach = todo.pop()
        for h, v in todo:
            self.eng[e].wait_ge(h, v)
        return attach

    def _commit(self, tok, reads, writes):
        for r in reads:
            r.rs.append(tok)
        for r in writes:
            r.w = tok
            r.rs = []

    def op(self, e, fn, reads=(), writes=()):
        if self.rec is not None:
            self.rec.append(("op", (e, fn), tuple(reads), tuple(writes)))
            return
        attach = self._deps(e, reads, writes, attach_ok=True)
        r = fn()
        first, ins = r if isinstance(r, tuple) else (r, r)
        if attach is not None:
            first._wait_ge(attach[0], attach[1])
        self.cnt[e] += 1
        ins.then_inc(self.sem[e], 1)
        tok = (self.sem[e], self.cnt[e], e)
        self._commit(tok, reads, writes)

    def play(self, ent):
        kind, a, reads, writes = ent
        saved, self.rec = self.rec, None
        if kind == "op":
            self.op(a[0], a[1], reads, writes)
        else:
            self.dma(a[0], a[1], a[2], a[3], reads, writes)
        self.rec = saved

    def dma(self, q, out, in_, dsem, reads=(), writes=()):
        if self.rec is not None:
            self.rec.append(("dma", (q, out, in_, dsem), tuple(reads), tuple(writes)))
            return
        self._deps(q, reads, writes)
        ins = self.eng[q].dma_start(out=out, in_=in_)
        dsem.v += 16
        ins.then_inc(dsem.h, 16)
        tok = (dsem.h, dsem.v, None)
        self._commit(tok, reads, writes)


def reorder(ents):
    n = len(ents)
    last_w, readers = {}, {}
    preds = [set() for _ in range(n)]
    for i, ent in enumerate(ents):
        reads, writes = ent[2], ent[3]
        for r in reads:
            if id(r) in last_w:
                preds[i].add(last_w[id(r)])
        for r in writes:
            if id(r) in last_w:
                preds[i].add(last_w[id(r)])
            preds[i].update(readers.get(id(r), ()))
        for r in reads:
            readers.setdefault(id(r), []).append(i)
        for r in writes:
            last_w[id(r)] = i
            readers[id(r)] = []
    dur = {"pe": 1.0, "act": 0.6, "dve": 0.45, "pool": 1.0, "sp": 0.1}
    free = {k: 0.0 for k in dur}
    finish = [None] * n
    done = [False] * n
    order = []
    pending = list(range(n))
    while pending:
        best, best_t = None, None
        for i in pending[:400]:
            if any(not done[p_] for p_ in preds[i]):
                continue
            eng = ents[i][1][0]
            ready = max([finish[p_] + (0.0 if ents[p_][1][0] == eng else 0.2) for p_ in preds[i]], default=0.0)
            t = max(ready, free[eng])
            if best is None or t < best_t - 1e-9:
                best, best_t = i, t
        i = best
        eng = ents[i][1][0]
        lat = dur[eng] if ents[i][0] == "op" else 3.0
        free[eng] = best_t + (dur[eng] if ents[i][0] == "op" else (0.1 if eng == "sp" else 1.0))
        finish[i] = best_t + lat
        done[i] = True
        order.append(i)
        pending.remove(i)
    return [ents[i] for i in order]


def host_constants():
    k = np.arange(128)
    c = {}
    c["c_ident"] = np.eye(128, dtype=np.float32)
    c["c_nu"] = -(k[:, None] >= k[None, :]).astype(np.float32)
    c["c_nl"] = -(k[:, None] < k[None, :]).astype(np.float32)
    c["c_mb"] = (k[:, None] < k[None, :]).astype(np.float32)
    wins = np.array([2, 4, 8, 16], np.float32)
    invw = np.zeros((128, 2), np.float32)
    corr = np.ones((128, 2, 16), np.float32)
    t = np.arange(16)
    for j in range(2):
        for half in range(2):
            w = wins[2 * j + half]
            invw[half * 64:(half + 1) * 64, j] = 1.0 / w
            corr[half * 64:(half + 1) * 64, j, :] = w / np.minimum(t + 1, w)
    c["c_invw"] = invw
    c["c_corr"] = corr
    return c


def build_nc():
    nc = bass.Bass("TRN2", target_bir_lowering=False)

    def din(name, shape, dt=F32):
        return nc.dram_tensor(name, list(shape), dt, kind="ExternalInput").ap()

    x_d = din("x", [S_LEN, D])
    mem_d = din("mem", [256, D])
    win_d = din("w_in", [D, 3072])
    wkv_d = din("w_mem_kv", [D, 512])
    wout_d = din("w_out", [D, D])
    gfm_d = din("g_fm", [128, 8])
    mgfm_d = din("mg_fm", [128, 8])
    psfm_d = din("ps_fm", [128, 2])
    qgb_d = din("qg_b", [128, 64])
    kgb_d = din("kg_b", [128, 64])
    pw_d = din("pw", [128, 2, 64])
    cid_d = din("c_ident", [128, 128])
    cnu_d = din("c_nu", [128, 128])
    cnl_d = din("c_nl", [128, 128])
    cmb_d = din("c_mb", [128, 128])
    cinvw_d = din("c_invw", [128, 2])
    ccorr_d = din("c_corr", [128, 2, 16])
    out_d = nc.dram_tensor("out", [S_LEN, D], F32, kind="ExternalOutput").ap()

    S = Sched(nc)
    sb = nc.alloc_sbuf_tensor

    win = sb("win", [128, 8, 3072], BF16)
    wout = sb("wout", [128, 8, 1024], BF16)
    mixed = sb("mixed", [128, 8, 512], BF16)
    KT = sb("KT", [128, 4, S_LEN], BF16)
    Vz = sb("Vz", [128, 16, 4, 2, 128], BF16)
    QTz = sb("QTz", [128, 1, 8, 512], BF16)
    sg = sb("sg", [128, 8, 512], BF16)
    hT = sb("hT", [128, 8, 512], BF16)
    xtb = [sb(f"xt{i}", [128, 1024], F32) for i in range(2)]
    ubuf = sb("ubuf", [128, 2, 528], F32)
    pa = sb("pa", [128, 528], F32)
    pb = sb("pb", [128, 528], F32)
    pooled = sb("pooled", [128, 2, 512], BF16)
    junk = pooled[:].rearrange("p a b -> p (a b)")
    mq = [sb(f"mq{i}", [128, 256], F32) for i in range(2)]
    QNT = sb("QNT", [128, 2, 512], BF16)
    KNT = sb("KNT", [128, 2, 256], BF16)
    MV = sb("MV", [128, 2, 256], BF16)
    kf = mq[0]
    NE = 6
    e_t = [sb(f"e{i}", [128, 512], F32) for i in range(NE)]
    sp_t = [sb(f"sp{i}", [128, 512], BF16) for i in range(NE)]
    P_t = [sb(f"P{i}", [128, 512], F32) for i in range(2)]
    a_t = [sb(f"a{i}", [128, 512], BF16) for i in range(2)]
    pT = [sb("pT0", [128, 512], BF16)] * 2
    g1 = [sb("g1_0", [128, 512], F32), sb("g1_1", [128, 512], F32)]
    g2 = [sb("g2_0", [128, 512], F32), sb("g2_1", [128, 512], F32)]
    msq = g2[1][:, 0:256]
    ident_f = sb("ident_f", [128, 128], F32)
    m01 = sb("m01", [128, 128], F32)
    nu_r = sb("nu_r", [128, 128], BF16)
    nl_r = sb("nl_r", [128, 128], BF16)
    ones_bf = sb("ones_bf", [128, 64], BF16)
    invw = sb("invw", [128, 2], F32)
    corr = sb("corr", [128, 2, 16], F32)
    g_fm = sb("g_fm_s", [128, 8], F32)
    mg_fm = sb("mg_fm_s", [128, 8], F32)
    ps_fm = sb("ps_fm_s", [128, 2], F32)
    qg_b = sb("qg_b_s", [128, 64], F32)
    kg_b = sb("kg_b_s", [128, 64], F32)
    pw = sb("pw_s", [128, 2, 64], BF16)
    stat = sb("stat", [128, 64], F32)
    ps = nc.alloc_psum_tensor("ps", [128, 8, 512], F32)
    wkv = Vz[:, 12:16].rearrange("p a b c d -> p (a b c d)").rearrange("p (c e) -> p c e", e=512)

    class RD(dict):
        def __missing__(self, k):
            self[k] = Res(k)
            return self[k]

    class DD(dict):
        def __missing__(self, k):
            self[k] = DmaSem(nc, "d_" + str(k))
            return self[k]
    R = RD()
    ds = DD()
    bank = [[R[f"bank{i}L"], R[f"bank{i}H"]] for i in range(8)]

    for name, t, d in [("ident_f", ident_f, cid_d), ("invw", invw, cinvw_d), ("corr", corr, ccorr_d),
                       ("g_fm", g_fm, gfm_d), ("mg_fm", mg_fm, mgfm_d), ("ps_fm", ps_fm, psfm_d),
                       ("qg_b", qg_b, qgb_d), ("kg_b", kg_b, kgb_d)]:
        S.dma("sp", t[:], d, ds[name], writes=[R[name]])
    S.dma("pool", pw[:], pw_d, ds["pw"], writes=[R["pw"]])
    S.dma("sp", m01[:], cmb_d, ds["m01"], writes=[R["m01"]])
    S.dma("pool", nu_r[:], cnu_d, ds["nu_r"], writes=[R["nu_r"]])
    S.dma("pool", nl_r[:], cnl_d, ds["nl_r"], writes=[R["nl_r"]])
    S.op("dve", lambda: nc.vector.memset(ones_bf[:], 1.0), writes=[R["ones_bf"]])

    wkv_v = wkv_d.rearrange("(c p) e -> p c e", p=128)
    win_v = win_d.rearrange("(c p) e -> p c e", p=128)

    def load_win(blk):
        for hh in range(2):
            S.dma("pool", win[:, hh * 4:(hh + 1) * 4, blk * 512:(blk + 1) * 512],
                  win_v[:, hh * 4:(hh + 1) * 4, blk * 512:(blk + 1) * 512], ds[f"win{blk}"], writes=[R[f"win{blk}"]])
    for blk in [2, 3, 1]:
        load_win(blk)
    S.op("pool", lambda: nc.gpsimd.memset(QTz[:, 0], 0.0), writes=[R["QT0"]])
    S.op("pool", lambda: nc.gpsimd.memset(Vz[:, 0:4], 0.0), writes=[R["V0"]])
    for blk in [5, 0, 4]:
        load_win(blk)
    for hh in range(2):
        S.dma("pool", wkv[:, hh * 4:(hh + 1) * 4, 0:512], wkv_v[:, hh * 4:(hh + 1) * 4, :], ds["wkv"], writes=[R["V3"]])
    for t4 in range(1, 3):
        S.op("pool", lambda t4=t4: nc.gpsimd.memset(Vz[:, t4 * 4:(t4 + 1) * 4], 0.0), writes=[R[f"V{t4}"]])
    wout_v = wout_d.rearrange("(c p) e -> p c e", p=128)
    for hh in range(4):
        S.dma("pool", wout[:, hh * 2:(hh + 1) * 2, :], wout_v[:, hh * 2:(hh + 1) * 2, :], ds["wout"], writes=[R["wout"]])

    def winres(col):
        return R[f"win{col // 512}"]

    def rms_rows(src_ap, width, ss_ap, key):
        S.op("act", lambda: nc.scalar.activation(junk[:, 0:width], src_ap, AF.Square, accum_out=ss_ap),
             reads=[R[key]], writes=[R[key + "_ss"], R["pooled"]])
        S.op("act", lambda: nc.scalar.activation(ss_ap, ss_ap, AF.Ln, bias=EPS, scale=1.0 / width),
             reads=[R[key + "_ss"]], writes=[R[key + "_ss"]])
        S.op("act", lambda: nc.scalar.activation(ss_ap, ss_ap, AF.Exp, scale=-0.5),
             reads=[R[key + "_ss"]], writes=[R[key + "_ss"]])

    def norm_and_transpose(xt, xkey, ss_ap, gscale, col0, tbs):
        rms_rows(xt[:], 1024, ss_ap, xkey)
        S.op("dve", lambda: nc.vector.tensor_scalar(xt[:], xt[:], ss_ap, None, ALU.mult),
             reads=[R[xkey], R[xkey + "_ss"]], writes=[R[xkey]])
        for half in range(2):
            b = tbs[half]

            def tr(b=b, half=half):
                first = ins = None
                for cc in range(4):
                    c = half * 4 + cc
                    ins = nc.tensor.transpose(ps[:, b, cc * 128:(cc + 1) * 128], xt[:, c * 128:(c + 1) * 128], ident_f[:])
                    first = first or ins
                return first, ins
            S.op("pe", tr, reads=[R[xkey], R["ident_f"]], writes=bank[b])
            for cc in range(4):
                c = half * 4 + cc
                S.op("dve", lambda c=c, cc=cc, b=b: nc.vector.tensor_scalar(
                    hT[:, c, col0:col0 + 128], ps[:, b, cc * 128:(cc + 1) * 128], gscale[:, c:c + 1], None, ALU.mult),
                    reads=[*bank[b], R["gsc"]], writes=[R[f"hT{col0 // 128}"]])

    def mm8(out_ap, lhs_fn, rhs_fn, reads, b):
        for hf in range(2):
            def f(hf=hf):
                first = ins = None
                for c in range(4 * hf, 4 * hf + 4):
                    ins = nc.tensor.matmul(out_ap, lhs_fn(c), rhs_fn(c), start=(c == 0), stop=(c == 7))
                    first = first or ins
                return first, ins
            S.op("pe", f, reads=reads, writes=bank[b])

    R["gsc"].w = None
    def wait_params():
        S.op("dve", lambda: nc.vector.tensor_copy(stat[:, 60:61], g_fm[:, 0:1]),
             reads=[R["g_fm"], R["mg_fm"], R["ps_fm"], R["qg_b"], R["kg_b"], R["invw"], R["corr"], R["pw"]],
             writes=[R["gsc"]])
    wait_params()

    BGB = [5, 6, 7]
    bgi = [0]

    def nb():
        b = BGB[bgi[0] % len(BGB)]
        bgi[0] += 1
        return b

    def MS():
        def mtile(s):
            xt = xtb[s]
            xkey = f"xt{s}"
            S.dma("sp", xt[:], mem_d[s * 128:(s + 1) * 128, :], ds[xkey], writes=[R[xkey]])
            norm_and_transpose(xt, xkey, stat[:, s:s + 1], mg_fm, s * 128, [nb(), nb()])
        mtile(0)
        mtile(1)
        tb = nb()

        def mkv(s):
            b = nb()
            if b == tb:
                b = nb()
            mm8(ps[:, b, :], lambda c: hT[:, c, s * 128:(s + 1) * 128], lambda c: wkv[:, c, 0:512], [R[f"hT{s}"], R["V3"]], b)
            S.op("dve", lambda: nc.vector.tensor_copy(MV[:, s, :], ps[:, b, 256:512]), reads=bank[b], writes=[R["MV"]])
            S.op("dve", lambda: nc.vector.tensor_copy(kf[:], ps[:, b, 0:256]), reads=bank[b], writes=[R["mq0"]])
            S.op("dve", lambda: nc.vector.tensor_tensor(msq[:], kf[:], kf[:], ALU.mult), reads=[R["mq0"]], writes=[R["g2_1"]])
            kss = stat[:, 8 + 4 * s:12 + 4 * s]
            S.op("dve", lambda: nc.vector.tensor_reduce(kss, msq[:].rearrange("p (h d) -> p h d", d=64), AX.X, ALU.add),
                 reads=[R["g2_1"]], writes=[R["kss"]])
            S.op("act", lambda: nc.scalar.activation(kss, kss, AF.Ln, bias=EPS, scale=1.0 / 64), reads=[R["kss"]], writes=[R["kss"]])
            S.op("act", lambda: nc.scalar.activation(kss, kss, AF.Exp, scale=-0.5), reads=[R["kss"]], writes=[R["kss"]])

            def hn(h):
                S.op("dve", lambda: nc.vector.scalar_tensor_tensor(
                    kf[:, h * 64:(h + 1) * 64], kf[:, h * 64:(h + 1) * 64], kss[:, h:h + 1], kg_b[:], ALU.mult, ALU.mult),
                    reads=[R["mq0"], R["kss"], R["gsc"]], writes=[R["mq0"]])
            for h in range(4):
                hn(h)

            def trk():
                first = ins = None
                for pr in range(2):
                    ins = nc.tensor.transpose(ps[:, tb, pr * 256 + s * 128: pr * 256 + (s + 1) * 128],
                                              kf[:, pr * 128:(pr + 1) * 128], ident_f[:])
                    first = first or ins
                return first, ins
            S.op("pe", trk, reads=[R["mq0"], R["ident_f"]], writes=bank[tb])
        mkv(0)
        mkv(1)

        def kn(pr):
            S.op("dve", lambda: nc.vector.tensor_scalar(KNT[:, pr, :], ps[:, tb, pr * 256:(pr + 1) * 256], 0.125, None, ALU.mult),
                 reads=bank[tb], writes=[R["KNT"]])
        kn(0)
        kn(1)
        S.op("dve", lambda: nc.vector.memset(Vz[:, 12:16], 0.0), reads=[R["V3"]], writes=[R["V3"]])


    def PA(tt, with_q=True, part="all"):
        t0 = tt * 512
        qb = 0
        core = part in ("all", "core")
        rest = part in ("all", "rest")

        def xtile(s):
            xt = xtb[s % 2]
            xkey = f"xt{s % 2}"
            S.dma("sp", xt[:], x_d[t0 + s * 128: t0 + (s + 1) * 128, :], ds[xkey], writes=[R[xkey]])
            norm_and_transpose(xt, xkey, stat[:, 16 + s:17 + s], g_fm, s * 128, [nb(), nb()])
        if core:
            for s in range(4):
                xtile(s)

        def kproj(j):
            b = nb()
            col = 1024 + j * 128
            mm8(ps[:, b, :], lambda c: win[:, c, col:col + 128], lambda c: hT[:, c, :], [R["hT0"], R["hT1"], R["hT2"], R["hT3"], winres(col)], b)
            S.op("dve", lambda: nc.vector.tensor_scalar(KT[:, j, t0:t0 + 512], ps[:, b, :], 0.125, None, ALU.mult),
                 reads=bank[b], writes=[R[f"KT{tt}"]])

        def mqproj(s):
            b = nb()
            m = mq[s % 2]
            mk = f"mq{s % 2}"
            mm8(ps[:, b, 0:256], lambda c: hT[:, c, s * 128:(s + 1) * 128], lambda c: win[:, c, 2560:2816], [R[f"hT{s}"], winres(2560)], b)
            S.op("dve", lambda: nc.vector.tensor_copy(m[:], ps[:, b, 0:256]), reads=bank[b], writes=[R[mk]])
            S.op("dve", lambda: nc.vector.tensor_tensor(msq[:], m[:], m[:], ALU.mult), reads=[R[mk]], writes=[R["g2_1"]])
            qss = stat[:, 24 + 4 * s:28 + 4 * s]
            S.op("dve", lambda: nc.vector.tensor_reduce(qss, msq[:].rearrange("p (h d) -> p h d", d=64), AX.X, ALU.add),
                 reads=[R["g2_1"]], writes=[R["qss"]])
            S.op("act", lambda: nc.scalar.activation(qss, qss, AF.Ln, bias=EPS, scale=1.0 / 64), reads=[R["qss"]], writes=[R["qss"]])
            S.op("act", lambda: nc.scalar.activation(qss, qss, AF.Exp, scale=-0.5), reads=[R["qss"]], writes=[R["qss"]])

            def hn(h):
                S.op("dve", lambda: nc.vector.scalar_tensor_tensor(
                    m[:, h * 64:(h + 1) * 64], m[:, h * 64:(h + 1) * 64], qss[:, h:h + 1], qg_b[:], ALU.mult, ALU.mult),
                    reads=[R[mk], R["qss"], R["gsc"]], writes=[R[mk]])
            for h in range(4):
                hn(h)

        def trq(s):
            m = mq[s % 2]
            mk = f"mq{s % 2}"
            b = nb()

            def f():
                first = ins = None
                for pr in range(2):
                    ins = nc.tensor.transpose(ps[:, b, pr * 128:(pr + 1) * 128], m[:, pr * 128:(pr + 1) * 128], ident_f[:])
                    first = first or ins
                return first, ins
            S.op("pe", f, reads=[R[mk], R["ident_f"]], writes=bank[b])
            S.op("dve", lambda: nc.vector.tensor_copy(QNT[:, :, s * 128:(s + 1) * 128],
                                                      ps[:, b, 0:256].rearrange("p (r t) -> p r t", r=2)),
                 reads=bank[b], writes=[R["QNT"]])

        if core:
            for j in range(4):
                kproj(j)

        def vproj(s):
            b = nb()
            mm8(ps[:, b, :], lambda c: hT[:, c, s * 128:(s + 1) * 128], lambda c: win[:, c, 1536:2048], [R[f"hT{s}"], winres(1536)], b)

            def ev(hh):
                S.op("dve", lambda: nc.vector.tensor_copy(
                    Vz[:, tt * 4 + s, :, hh, hh * 64:(hh + 1) * 64],
                    ps[:, b, :].rearrange("p (j h d) -> p j h d", h=2, d=64)[:, :, hh, :]), reads=bank[b], writes=[R[f"V{tt}"]])
            ev(0)
            ev(1)
        if core:
            for s in range(4):
                vproj(s)

        def qproj(j):
            b = nb()
            col = 512 + j * 128
            mm8(ps[:, b, :], lambda c: win[:, c, col:col + 128], lambda c: hT[:, c, :], [R["hT0"], R["hT1"], R["hT2"], R["hT3"], winres(col)], b)

            def ev(hh):
                S.op("dve", lambda: nc.vector.tensor_copy(QTz[hh * 64:(hh + 1) * 64, qb, 2 * j + hh, :], ps[hh * 64:(hh + 1) * 64, b, :]),
                     reads=bank[b], writes=[R[f"QT{qb}"]])
            ev(0)
            ev(1)
        if with_q and core:
            for j in range(4):
                qproj(j)

        def uproj(j):
            if tt == 0:
                S.op("dve", lambda: nc.vector.memset(ubuf[:, j, 0:16], 0.0), writes=[R["ubuf"]])
            else:
                S.op("dve", lambda: nc.vector.tensor_copy(ubuf[:, j, 0:16], ubuf[:, j, 512:528]), reads=[R["ubuf"]], writes=[R["ubuf"]])
            b = nb()
            col = j * 128
            mm8(ps[:, b, :], lambda c: win[:, c, col:col + 128], lambda c: hT[:, c, :], [R["hT0"], R["hT1"], R["hT2"], R["hT3"], winres(col)], b)
            S.op("dve", lambda: nc.vector.tensor_copy(ubuf[:, j, 16:528], ps[:, b, :]), reads=bank[b], writes=[R["ubuf"]])
        if rest:
            mqproj(0); mqproj(1); trq(0); mqproj(2); trq(1); mqproj(3); trq(2)
            for j in range(2):
                uproj(j)
            trq(3)

    def QP(tt):
        def qproj(j):
            b = nb()
            col = 512 + j * 128
            mm8(ps[:, b, :], lambda c: win[:, c, col:col + 128], lambda c: hT[:, c, :], [R["hT0"], R["hT1"], R["hT2"], R["hT3"], winres(col)], b)

            def ev(hh):
                S.op("dve", lambda: nc.vector.tensor_copy(QTz[hh * 64:(hh + 1) * 64, 0, 2 * j + hh, :], ps[hh * 64:(hh + 1) * 64, b, :]),
                     reads=bank[b], writes=[R["QT0"]])
            ev(0)
            ev(1)
        for j in range(4):
            qproj(j)

    def G(tt):
        order = [(2, 2048), (3, 2176), (4, 2304), (5, 2432), (0, 256), (1, 384), (6, 2816), (7, 2944)]

        def one(gi, mc, gcol):
            gb = nb()
            mm8(ps[:, gb, :], lambda c: win[:, c, gcol:gcol + 128], lambda c: hT[:, c, :], [R["hT0"], R["hT1"], R["hT2"], R["hT3"], winres(gcol)], gb)
            t1, t2 = g1[gi % 2], g2[gi % 2]
            k1, k2 = f"g1_{gi % 2}", f"g2_{gi % 2}"
            S.op("act", lambda: nc.scalar.activation(t1[:], ps[:, gb, :], AF.Exp, scale=-1.0), reads=bank[gb], writes=[R[k1]])
            S.op("act", lambda: nc.scalar.activation(t1[:], t1[:], AF.Ln, bias=1.0), reads=[R[k1]], writes=[R[k1]])
            S.op("act", lambda: nc.scalar.activation(t2[:], t1[:], AF.Exp, scale=-1.0), reads=[R[k1]], writes=[R[k2]])
            S.op("dve", lambda: nc.vector.tensor_tensor(sg[:, mc, :], ps[:, gb, :], t2[:], ALU.mult),
                 reads=[*bank[gb], R[k2]], writes=[R[f"sg{mc}"]])
        for gi, (mc, gcol) in enumerate(order):
            one(gi, mc, gcol)

    def PM(tt):
        def pool_chunk(j):
            S.op("dve", lambda: nc.vector.tensor_tensor(pa[:, 1:528], ubuf[:, j, 1:528], ubuf[:, j, 0:527], ALU.add),
                 reads=[R["ubuf"]], writes=[R["pa"]])
            S.op("dve", lambda: nc.vector.tensor_tensor(pb[:, 3:528], pa[:, 3:528], pa[:, 1:526], ALU.add),
                 reads=[R["pa"]], writes=[R["pb"]])
            if j == 1:
                S.op("dve", lambda: nc.vector.tensor_tensor(pa[:, 7:528], pb[:, 7:528], pb[:, 3:524], ALU.add),
                     reads=[R["pb"]], writes=[R["pa"]])
                S.op("dve", lambda: nc.vector.tensor_tensor(pb[64:128, 15:528], pa[64:128, 15:528], pa[64:128, 7:520], ALU.add),
                     reads=[R["pa"]], writes=[R["pb"]])
            if tt == 0:
                S.op("dve", lambda: nc.vector.tensor_tensor(pa[0:64, 16:32], pa[0:64, 16:32], corr[0:64, j, :], ALU.mult),
                     reads=[R["pa"], R["gsc"]], writes=[R["pa"]])
                S.op("dve", lambda: nc.vector.tensor_tensor(pb[64:128, 16:32], pb[64:128, 16:32], corr[64:128, j, :], ALU.mult),
                     reads=[R["pb"], R["gsc"]], writes=[R["pb"]])
            S.op("dve", lambda: nc.vector.scalar_tensor_tensor(pooled[0:64, j, :], pa[0:64, 16:528], invw[0:64, j:j + 1],
                                                               ubuf[0:64, j, 16:528], ALU.mult, ALU.subtract),
                 reads=[R["pa"], R["ubuf"], R["gsc"]], writes=[R["pooled"]])
            S.op("dve", lambda: nc.vector.scalar_tensor_tensor(pooled[64:128, j, :], pb[64:128, 16:528], invw[64:128, j:j + 1],
                                                               ubuf[64:128, j, 16:528], ALU.mult, ALU.subtract),
                 reads=[R["pb"], R["ubuf"], R["gsc"]], writes=[R["pooled"]])
            yb = nb()

            def ymm():
                first = ins = None
                for half in range(2):
                    r0 = half * 64
                    ins = nc.tensor.matmul(ps[r0:r0 + 64, yb, :], pw[r0:r0 + 64, j, :], pooled[r0:r0 + 64, j, :], start=True, stop=True)
                    first = first or ins
                return first, ins
            S.op("pe", ymm, reads=[R["pooled"], R["gsc"]], writes=bank[yb])
            S.op("dve", lambda: nc.vector.scalar_tensor_tensor(mixed[:, j, :], ps[:, yb, :], ps_fm[:, j:j + 1], sg[:, j, :], ALU.mult, ALU.mult),
                 reads=[*bank[yb], R[f"sg{j}"], R["gsc"]], writes=[R["mixed"]])
        pool_chunk(0)
        pool_chunk(1)

        def mem_pair(pr):
            nbk, dbk, sbk = nb(), nb(), nb()

            def head(hh):
                hm = 2 * pr + hh
                r0 = hh * 64

                def chunk(s2_):
                    S.op("pe", lambda: nc.tensor.matmul(ps[:, sbk, :], KNT[r0:r0 + 64, pr, s2_ * 128:(s2_ + 1) * 128], QNT[r0:r0 + 64, pr, :],
                                                        start=True, stop=True),
                         reads=[R["KNT"], R["QNT"]], writes=bank[sbk])
                    pt = pT[0]
                    S.op("act", lambda: nc.scalar.activation(pt[:], ps[:, sbk, :], AF.Exp, bias=-8.0), reads=bank[sbk], writes=[R["pT0"]])
                    S.op("pe", lambda: nc.tensor.matmul(ps[r0:r0 + 64, nbk, :], MV[:, s2_, hm * 64:(hm + 1) * 64], pt[:],
                                                        start=(s2_ == 0), stop=(s2_ == 1)),
                         reads=[R["pT0"], R["MV"]], writes=[bank[nbk][hh]])
                    S.op("pe", lambda: nc.tensor.matmul(ps[r0:r0 + 64, dbk, :], ones_bf[:], pt[:], start=(s2_ == 0), stop=(s2_ == 1)),
                         reads=[R["pT0"], R["ones_bf"]], writes=[bank[dbk][hh]])
                chunk(0)
                chunk(1)
            head(0)
            head(1)
            rd = g2[0]
            S.op("act", lambda: nc.scalar.activation(rd[:], ps[:, dbk, :], AF.Ln), reads=bank[dbk], writes=[R["g2_0"]])
            S.op("act", lambda: nc.scalar.activation(rd[:], rd[:], AF.Exp, scale=-1.0), reads=[R["g2_0"]], writes=[R["g2_0"]])
            S.op("dve", lambda: nc.vector.tensor_tensor(rd[:], ps[:, nbk, :], rd[:], ALU.mult), reads=[*bank[nbk], R["g2_0"]], writes=[R["g2_0"]])
            S.op("dve", lambda: nc.vector.tensor_tensor(mixed[:, 6 + pr, :], rd[:], sg[:, 6 + pr, :], ALU.mult),
                 reads=[R["g2_0"], R[f"sg{6 + pr}"]], writes=[R["mixed"]])
        mem_pair(0)
        mem_pair(1)

    def OP(tt):
        t0 = tt * 512

        def sub(s):
            xr = xtb[s % 2]
            xk = f"xt{s % 2}"
            S.dma("sp", xr[:], x_d[t0 + s * 128: t0 + (s + 1) * 128, :], ds[xk + "_ld"], writes=[R[xk]])

            def half(nh):
                b = nb()
                mm8(ps[:, b, :], lambda c: mixed[:, c, s * 128:(s + 1) * 128], lambda c: wout[:, c, nh * 512:(nh + 1) * 512],
                    [R["mixed"], R["wout"]], b)
                S.op("dve", lambda: nc.vector.tensor_tensor(xr[:, nh * 512:(nh + 1) * 512], ps[:, b, :], xr[:, nh * 512:(nh + 1) * 512], ALU.add),
                     reads=[*bank[b], R[xk]], writes=[R[xk]])
            half(0)
            half(1)
            S.dma("pool", out_d[t0 + s * 128: t0 + (s + 1) * 128, :], xr[:], ds[xk + "_st"], reads=[R[xk]])
        for s in range(4):
            sub(s)

    def AT(tt, bg, bg_tail):
        nkb = 4 * tt + 4
        qb = 0
        items = [(2 * pair + hh, kb) for pair in range(4) for kb in reversed(range(nkb)) for hh in range(2)]
        n_it = len(items)

        def geom(n):
            h, kb = items[n]
            jd = kb - 4 * tt
            c0 = max(jd, 0) * 128
            return h, kb, h // 2, c0, (jd >= 0)

        def s1a(n):
            h, kb, j, c0, diag = geom(n)
            zb = n % 2

            def f():
                return nc.tensor.matmul(ps[:, zb, c0:512], KT[:, j, kb * 128:(kb + 1) * 128],
                                        QTz[:, qb, h, c0:512], start=True, stop=True)
            S.op("pe", f, reads=[R[f"KT{kb // 4}"], R[f"QT{qb}"]], writes=bank[zb])
            S.op("act", lambda: nc.scalar.activation(e_t[n % NE][:, c0:512], ps[:, zb, c0:512], AF.Exp),
                 reads=bank[zb], writes=[R[f"e{n % NE}"]])

        def s1m(n):
            h, kb, j, c0, diag = geom(n)
            if diag:
                S.op("dve", lambda: nc.vector.tensor_tensor(e_t[n % NE][:, c0:c0 + 128], e_t[n % NE][:, c0:c0 + 128], m01[:], ALU.mult),
                     reads=[R[f"e{n % NE}"], R["m01"]], writes=[R[f"e{n % NE}"]])

        def s1b(n):
            h, kb, j, c0, diag = geom(n)
            S.op("act", lambda: nc.scalar.activation(sp_t[n % NE][:, c0:512], e_t[n % NE][:, c0:512], AF.Ln, bias=1.0),
                 reads=[R[f"e{n % NE}"]], writes=[R[f"sp{n % NE}"]])

        def s2nu(n):
            h, kb, j, c0, diag = geom(n)
            ab = 2 + h % 2
            S.op("pe", lambda: nc.tensor.matmul(ps[:, ab, c0:512], nu_r[:], sp_t[n % NE][:, c0:512], start=(kb == nkb - 1), stop=(kb == 0),
                                                skip_group_check=True),
                 reads=[R[f"sp{n % NE}"], R["nu_r"]], writes=bank[ab])

        def s2p(n):
            h, kb, j, c0, diag = geom(n)
            ab = 2 + h % 2
            S.op("act", lambda: nc.scalar.activation(P_t[n % 2][:, c0:512], ps[:, ab, c0:512], AF.Exp),
                 reads=bank[ab], writes=[R[f"P{n % 2}"]])

        def s2b(n):
            h, kb, j, c0, diag = geom(n)
            ab = 2 + h % 2
            S.op("dve", lambda: nc.vector.tensor_tensor(a_t[n % 2][:, c0:512], e_t[n % NE][:, c0:512], P_t[n % 2][:, c0:512], ALU.mult),
                 reads=[R[f"e{n % NE}"], R[f"P{n % 2}"]], writes=[R[f"a{n % 2}"]])
            if kb != 0:
                S.op("pe", lambda: nc.tensor.matmul(ps[:, ab, c0:512], nl_r[:], sp_t[n % NE][:, c0:512], start=False, stop=False,
                                                    skip_group_check=True),
                     reads=[R[f"sp{n % NE}"], R["nl_r"]], writes=bank[ab])

        def s2c(n):
            h, kb, j, c0, diag = geom(n)
            first = (kb == nkb - 1 and h % 2 == 0)
            last = (kb == 0 and h % 2 == 1)
            S.op("pe", lambda: nc.tensor.matmul(ps[:, 4, c0:512], Vz[:, kb, j, h % 2, :], a_t[n % 2][:, c0:512],
                                                start=first, stop=last, skip_group_check=True),
                 reads=[R[f"a{n % 2}"], R[f"V{kb // 4}"]], writes=bank[4])
            if last:
                need_bg(lambda ent: any(r is R[f"sg{2 + j}"] for r in ent[3]) or any(r is R["mixed"] for r in ent[2]))
                S.op("dve", lambda: nc.vector.tensor_tensor(mixed[:, 2 + j, :], ps[:, 4, :], sg[:, 2 + j, :], ALU.mult),
                     reads=[*bank[4], R[f"sg{2 + j}"]], writes=[R["mixed"]])

        def ok(n):
            return 0 <= n < n_it
        pos = 0
        tpos = 0

        def need_bg(pred):
            nonlocal pos
            idx = None
            for k in range(pos, len(bg)):
                if pred(bg[k]):
                    idx = k
            if idx is not None:
                while pos <= idx:
                    S.play(bg[pos])
                    pos += 1

        def blocked(ent, fresh):
            eng = ent[1][0]
            hit = [fresh[id(r)] for r in ent[2] + ent[3] if id(r) in fresh]
            if not hit:
                return False
            if eng == "act":
                return tt == NT - 1 and any(h != "act" for h in hit)
            if eng == "pe":
                return tt == NT - 1 and any(h != "pe" for h in hit)
            return False
        quota = (13 * len(bg)) // (10 * n_it) + 2
        for i in range(-6, n_it):
            if ok(i + 6):
                s1a(i + 6)
            if ok(i + 5):
                s1m(i + 5)
            if ok(i + 4):
                s1b(i + 4)
            if ok(i + 1):
                s2b(i + 1)
            if ok(i + 3):
                s2nu(i + 3)
            if ok(i + 2):
                s2p(i + 2)
            if ok(i):
                s2c(i)
            fresh = {}
            cnt = 0
            while pos < len(bg) and cnt < quota:
                ent = bg[pos]
                if blocked(ent, fresh):
                    break
                S.play(ent)
                for r in ent[3]:
                    fresh[id(r)] = ent[1][0]
                pos += 1
                cnt += 1
            while pos >= len(bg) and i + 6 >= n_it - 1 and tpos < len(bg_tail) and cnt < quota + 3:
                ent = bg_tail[tpos]
                if blocked(ent, fresh):
                    break
                S.play(ent)
                for r in ent[3]:
                    fresh[id(r)] = ent[1][0]
                tpos += 1
                cnt += 1
        while pos < len(bg):
            S.play(bg[pos])
            pos += 1
        while tpos < len(bg_tail):
            S.play(bg_tail[tpos])
            tpos += 1

    S.rec = []
    PA(0, part="core")
    fg0, S.rec = reorder(S.rec), None
    for ent in fg0:
        S.play(ent)
    for tt in range(NT):
        S.rec = []
        if tt == 0:
            PA(0, part="rest")
        if tt > 0:
            OP(tt - 1)
        G(tt)
        if tt == 0:
            MS()
        PM(tt)
        if tt + 1 < NT:
            PA(tt + 1, with_q=False)
        bg, S.rec = reorder(S.rec), []
        if tt + 1 < NT:
            QP(tt + 1)
        bg_tail, S.rec = reorder(S.rec), None
        AT(tt, bg, bg_tail)
    BGB[:] = [5, 6, 7, 0, 1, 2, 3, 4]
    S.rec = []
    OP(NT - 1)
    fin, S.rec = reorder(S.rec), None
    for ent in fin:
        S.play(ent)

    for i in range(2):
        d = ds[f"xt{i}_st"]
        nc.gpsimd.wait_ge(d.h, d.v)
    return nc


_CACHE = {}


def kernel(x, mem, norm_g, w_in, pool_w, pool_scale, mem_norm_g, w_mem_kv, q_norm_g, k_norm_g, w_out):
    B = x.shape[0]
    f = np.float32
    x = np.ascontiguousarray(x, dtype=f)
    mem = np.ascontiguousarray(mem, dtype=f)
    shared = host_constants()
    shared["w_in"] = np.ascontiguousarray(w_in[0], dtype=f)
    shared["w_mem_kv"] = np.ascontiguousarray(w_mem_kv[0], dtype=f)
    shared["w_out"] = np.ascontiguousarray(w_out[0], dtype=f)
    shared["g_fm"] = np.ascontiguousarray(np.asarray(norm_g[0], f).reshape(8, 128).T)
    shared["mg_fm"] = np.ascontiguousarray(np.asarray(mem_norm_g[0], f).reshape(8, 128).T)
    shared["ps_fm"] = np.ascontiguousarray(np.asarray(pool_scale[0], f).reshape(2, 128).T)
    shared["qg_b"] = np.ascontiguousarray(np.broadcast_to(np.asarray(q_norm_g[0], f)[None, :], (128, 64)))
    shared["kg_b"] = np.ascontiguousarray(np.broadcast_to(np.asarray(k_norm_g[0], f)[None, :], (128, 64)))
    shared["pw"] = np.ascontiguousarray(np.asarray(pool_w[0], f).reshape(2, 2, 64, 64).transpose(1, 2, 0, 3).reshape(128, 2, 64))
    if "nc" not in _CACHE:
        _CACHE["nc"] = build_nc()
    nc = _CACHE["nc"]
    in_maps = []
    for b in range(B):
        m = dict(shared)
        m["x"] = x[b]
        m["mem"] = mem[b]
        in_maps.append(m)
    res = run_bass_kernel_spmd(nc, in_maps, core_ids=list(range(B)))
    return np.stack([np.asarray(r["out"], dtype=f) for r in res.results], axis=0)
```

```python
import numpy as np
import concourse.bass as bass
import concourse.mybir as mybir
from concourse.bass_utils import run_bass_kernel_spmd

F32 = mybir.dt.float32
F32R = mybir.dt.float32r
BF16 = mybir.dt.bfloat16
AF = mybir.ActivationFunctionType
ALU = mybir.AluOpType
AX = mybir.AxisListType

S_LEN = 2048
D = 1024
NT = 4
EPS = 1e-6


class Res:
    __slots__ = ("name", "w", "rs")

    def __init__(self, name):
        self.name = name
        self.w = None
        self.rs = []


class DmaSem:
    def __init__(self, nc, name):
        self.h = nc.alloc_semaphore(name)
        self.v = 0


class Sched:
    def __init__(self, nc):
        self.nc = nc
        self.eng = {"pe": nc.tensor, "act": nc.scalar, "dve": nc.vector,
                    "pool": nc.gpsimd, "sp": nc.sync}
        self.sem = {k: nc.alloc_semaphore("prog_" + k) for k in self.eng}
        self.cnt = {k: 0 for k in self.eng}
        self.waited = {}
        self.rec = None

    def _deps(self, e, reads, writes, attach_ok=False):
        need = {}
        selfv = 0

        def add(tok, kind):
            nonlocal selfv
            if tok is None:
                return
            h, v, pe = tok
            if pe == e:
                if e != "pe" and kind != "waw":
                    selfv = max(selfv, v)
                return
            key = id(h)
            if need.get(key, (None, 0))[1] < v:
                need[key] = (h, v)
        for r in reads:
            add(r.w, "raw")
        for r in writes:
            add(r.w, "waw")
            for t in r.rs:
                add(t, "war")
        todo = []
        for key, (h, v) in need.items():
            wk = (e, key)
            if self.waited.get(wk, 0) >= v:
                continue
            self.waited[wk] = v
            todo.append((h, v))
        attach = None
        if selfv > self.waited.get((e, "self"), 0):
            self.waited[(e, "self")] = selfv
            attach = (self.sem[e], selfv)
        elif attach_ok and todo:
            attach = todo.pop()
        for h, v in todo:
            self.eng[e].wait_ge(h, v)
        return attach

    def _commit(self, tok, reads, writes):
        for r in reads:
            r.rs.append(tok)
        for r in writes:
            r.w = tok
            r.rs = []

    def op(self, e, fn, reads=(), writes=()):
        if self.rec is not None:
            self.rec.append(("op", (e, fn), tuple(reads), tuple(writes)))
            return
        attach = self._deps(e, reads, writes, attach_ok=True)
        r = fn()
        first, ins = r if isinstance(r, tuple) else (r, r)
        if attach is not None:
            first._wait_ge(attach[0], attach[1])
        self.cnt[e] += 1
        ins.then_inc(self.sem[e], 1)
        tok = (self.sem[e], self.cnt[e], e)
        self._commit(tok, reads, writes)

    def play(self, ent):
        kind, a, reads, writes = ent
        saved, self.rec = self.rec, None
        if kind == "op":
            self.op(a[0], a[1], reads, writes)
        else:
            self.dma(a[0], a[1], a[2], a[3], reads, writes)
        self.rec = saved

    def dma(self, q, out, in_, dsem, reads=(), writes=()):
        if self.rec is not None:
            self.rec.append(("dma", (q, out, in_, dsem), tuple(reads), tuple(writes)))
            return
        self._deps(q, reads, writes)
        ins = self.eng[q].dma_start(out=out, in_=in_)
        dsem.v += 16
        ins.then_inc(dsem.h, 16)
        tok = (dsem.h, dsem.v, None)
        self._commit(tok, reads, writes)


def reorder(ents):
    n = len(ents)
    last_w, readers = {}, {}
    preds = [set() for _ in range(n)]
    for i, ent in enumerate(ents):
        reads, writes = ent[2], ent[3]
        for r in reads:
            if id(r) in last_w:
                preds[i].add(last_w[id(r)])
        for r in writes:
            if id(r) in last_w:
                preds[i].add(last_w[id(r)])
            preds[i].update(readers.get(id(r), ()))
        for r in reads:
            readers.setdefault(id(r), []).append(i)
        for r in writes:
            last_w[id(r)] = i
            readers[id(r)] = []
    dur = {"pe": 1.0, "act": 0.6, "dve": 0.45, "pool": 1.0, "sp": 0.1}
    free = {k: 0.0 for k in dur}
    finish = [None] * n
    done = [False] * n
    order = []
    pending = list(range(n))
    while pending:
        best, best_t = None, None
        for i in pending[:400]:
            if any(not done[p_] for p_ in preds[i]):
                continue
            eng = ents[i][1][0]
            ready = max([finish[p_] + (0.0 if ents[p_][1][0] == eng else 0.2) for p_ in preds[i]], default=0.0)
            t = max(ready, free[eng])
            if best is None or t < best_t - 1e-9:
                best, best_t = i, t
        i = best
        eng = ents[i][1][0]
        lat = dur[eng] if ents[i][0] == "op" else 3.0
        free[eng] = best_t + (dur[eng] if ents[i][0] == "op" else (0.1 if eng == "sp" else 1.0))
        finish[i] = best_t + lat
        done[i] = True
        order.append(i)
        pending.remove(i)
    return [ents[i] for i in order]


def host_constants():
    k = np.arange(128)
    c = {}
    c["c_ident"] = np.eye(128, dtype=np.float32)
    c["c_nu"] = -(k[:, None] >= k[None, :]).astype(np.float32)
    c["c_nl"] = -(k[:, None] < k[None, :]).astype(np.float32)
    c["c_mb"] = (k[:, None] < k[None, :]).astype(np.float32)
    wins = np.array([2, 4, 8, 16], np.float32)
    invw = np.zeros((128, 2), np.float32)
    corr = np.ones((128, 2, 16), np.float32)
    t = np.arange(16)
    for j in range(2):
        for half in range(2):
            w = wins[2 * j + half]
            invw[half * 64:(half + 1) * 64, j] = 1.0 / w
            corr[half * 64:(half + 1) * 64, j, :] = w / np.minimum(t + 1, w)
    c["c_invw"] = invw
    c["c_corr"] = corr
    return c


def build_nc():
    nc = bass.Bass("TRN2", target_bir_lowering=False)

    def din(name, shape, dt=F32):
        return nc.dram_tensor(name, list(shape), dt, kind="ExternalInput").ap()

    x_d = din("x", [S_LEN, D])
    mem_d = din("mem", [256, D])
    win_d = din("w_in", [D, 3072])
    wkv_d = din("w_mem_kv", [D, 512])
    wout_d = din("w_out", [D, D])
    gfm_d = din("g_fm", [128, 8])
    mgfm_d = din("mg_fm", [128, 8])
    psfm_d = din("ps_fm", [128, 2])
    qgb_d = din("qg_b", [128, 64])
    kgb_d = din("kg_b", [128, 64])
    pw_d = din("pw", [128, 2, 64])
    cid_d = din("c_ident", [128, 128])
    cnu_d = din("c_nu", [128, 128])
    cnl_d = din("c_nl", [128, 128])
    cmb_d = din("c_mb", [128, 128])
    cinvw_d = din("c_invw", [128, 2])
    ccorr_d = din("c_corr", [128, 2, 16])
    out_d = nc.dram_tensor("out", [S_LEN, D], F32, kind="ExternalOutput").ap()

    S = Sched(nc)
    sb = nc.alloc_sbuf_tensor

    win = sb("win", [128, 8, 3072], BF16)
    wout = sb("wout", [128, 8, 1024], BF16)
    mixed = sb("mixed", [128, 8, 512], BF16)
    KT = sb("KT", [128, 4, S_LEN], BF16)
    Vz = sb("Vz", [128, 16, 4, 2, 128], BF16)
    QTz = sb("QTz", [128, 1, 8, 512], BF16)
    sg = sb("sg", [128, 8, 512], BF16)
    hT = sb("hT", [128, 8, 512], BF16)
    xtb = [sb(f"xt{i}", [128, 1024], F32) for i in range(2)]
    ubuf = sb("ubuf", [128, 2, 528], F32)
    pa = sb("pa", [128, 528], F32)
    pb = sb("pb", [128, 528], F32)
    pooled = sb("pooled", [128, 2, 512], BF16)
    junk = pooled[:].rearrange("p a b -> p (a b)")
    mq = [sb(f"mq{i}", [128, 256], F32) for i in range(2)]
    QNT = sb("QNT", [128, 2, 512], BF16)
    KNT = sb("KNT", [128, 2, 256], BF16)
    MV = sb("MV", [128, 2, 256], BF16)
    kf = mq[0]
    NE = 6
    e_t = [sb(f"e{i}", [128, 512], F32) for i in range(NE)]
    sp_t = [sb(f"sp{i}", [128, 512], BF16) for i in range(NE)]
    P_t = [sb(f"P{i}", [128, 512], F32) for i in range(2)]
    a_t = [sb(f"a{i}", [128, 512], BF16) for i in range(2)]
    pT = [sb("pT0", [128, 512], BF16)] * 2
    g1 = [sb("g1_0", [128, 512], F32), sb("g1_1", [128, 512], F32)]
    g2 = [sb("g2_0", [128, 512], F32), sb("g2_1", [128, 512], F32)]
    msq = g2[1][:, 0:256]
    ident_f = sb("ident_f", [128, 128], F32)
    m01 = sb("m01", [128, 128], F32)
    nu_r = sb("nu_r", [128, 128], BF16)
    nl_r = sb("nl_r", [128, 128], BF16)
    ones_bf = sb("ones_bf", [128, 64], BF16)
    invw = sb("invw", [128, 2], F32)
    corr = sb("corr", [128, 2, 16], F32)
    g_fm = sb("g_fm_s", [128, 8], F32)
    mg_fm = sb("mg_fm_s", [128, 8], F32)
    ps_fm = sb("ps_fm_s", [128, 2], F32)
    qg_b = sb("qg_b_s", [128, 64], F32)
    kg_b = sb("kg_b_s", [128, 64], F32)
    pw = sb("pw_s", [128, 2, 64], BF16)
    stat = sb("stat", [128, 64], F32)
    ps = nc.alloc_psum_tensor("ps", [128, 8, 512], F32)
    wkv = Vz[:, 12:16].rearrange("p a b c d -> p (a b c d)").rearrange("p (c e) -> p c e", e=512)

    class RD(dict):
        def __missing__(self, k):
            self[k] = Res(k)
            return self[k]

    class DD(dict):
        def __missing__(self, k):
            self[k] = DmaSem(nc, "d_" + str(k))
            return self[k]
    R = RD()
    ds = DD()
    bank = [[R[f"bank{i}L"], R[f"bank{i}H"]] for i in range(8)]

    for name, t, d in [("ident_f", ident_f, cid_d), ("invw", invw, cinvw_d), ("corr", corr, ccorr_d),
                       ("g_fm", g_fm, gfm_d), ("mg_fm", mg_fm, mgfm_d), ("ps_fm", ps_fm, psfm_d),
                       ("qg_b", qg_b, qgb_d), ("kg_b", kg_b, kgb_d)]:
        S.dma("sp", t[:], d, ds[name], writes=[R[name]])
    S.dma("pool", pw[:], pw_d, ds["pw"], writes=[R["pw"]])
    S.dma("sp", m01[:], cmb_d, ds["m01"], writes=[R["m01"]])
    S.dma("pool", nu_r[:], cnu_d, ds["nu_r"], writes=[R["nu_r"]])
    S.dma("pool", nl_r[:], cnl_d, ds["nl_r"], writes=[R["nl_r"]])
    S.op("dve", lambda: nc.vector.memset(ones_bf[:], 1.0), writes=[R["ones_bf"]])

    wkv_v = wkv_d.rearrange("(c p) e -> p c e", p=128)
    win_v = win_d.rearrange("(c p) e -> p c e", p=128)

    def load_win(blk):
        for hh in range(2):
            S.dma("pool", win[:, hh * 4:(hh + 1) * 4, blk * 512:(blk + 1) * 512],
                  win_v[:, hh * 4:(hh + 1) * 4, blk * 512:(blk + 1) * 512], ds[f"win{blk}"], writes=[R[f"win{blk}"]])
    for blk in [2, 3, 1]:
        load_win(blk)
    S.op("pool", lambda: nc.gpsimd.memset(QTz[:, 0], 0.0), writes=[R["QT0"]])
    S.op("pool", lambda: nc.gpsimd.memset(Vz[:, 0:4], 0.0), writes=[R["V0"]])
    for blk in [5, 0, 4]:
        load_win(blk)
    for hh in range(2):
        S.dma("pool", wkv[:, hh * 4:(hh + 1) * 4, 0:512], wkv_v[:, hh * 4:(hh + 1) * 4, :], ds["wkv"], writes=[R["V3"]])
    for t4 in range(1, 3):
        S.op("pool", lambda t4=t4: nc.gpsimd.memset(Vz[:, t4 * 4:(t4 + 1) * 4], 0.0), writes=[R[f"V{t4}"]])
    wout_v = wout_d.rearrange("(c p) e -> p c e", p=128)
    for hh in range(4):
        S.dma("pool", wout[:, hh * 2:(hh + 1) * 2, :], wout_v[:, hh * 2:(hh + 1) * 2, :], ds["wout"], writes=[R["wout"]])

    def winres(col):
        return R[f"win{col // 512}"]

    def rms_rows(src_ap, width, ss_ap, key):
        S.op("act", lambda: nc.scalar.activation(junk[:, 0:width], src_ap, AF.Square, accum_out=ss_ap),
             reads=[R[key]], writes=[R[key + "_ss"], R["pooled"]])
        S.op("act", lambda: nc.scalar.activation(ss_ap, ss_ap, AF.Ln, bias=EPS, scale=1.0 / width),
             reads=[R[key + "_ss"]], writes=[R[key + "_ss"]])
        S.op("act", lambda: nc.scalar.activation(ss_ap, ss_ap, AF.Exp, scale=-0.5),
             reads=[R[key + "_ss"]], writes=[R[key + "_ss"]])

    def norm_and_transpose(xt, xkey, ss_ap, gscale, col0, tbs):
        rms_rows(xt[:], 1024, ss_ap, xkey)
        S.op("dve", lambda: nc.vector.tensor_scalar(xt[:], xt[:], ss_ap, None, ALU.mult),
             reads=[R[xkey], R[xkey + "_ss"]], writes=[R[xkey]])
        for half in range(2):
            b = tbs[half]

            def tr(b=b, half=half):
                first = ins = None
                for cc in range(4):
                    c = half * 4 + cc
                    ins = nc.tensor.transpose(ps[:, b, cc * 128:(cc + 1) * 128], xt[:, c * 128:(c + 1) * 128], ident_f[:])
                    first = first or ins
                return first, ins
            S.op("pe", tr, reads=[R[xkey], R["ident_f"]], writes=bank[b])
            for cc in range(4):
                c = half * 4 + cc
                S.op("dve", lambda c=c, cc=cc, b=b: nc.vector.tensor_scalar(
                    hT[:, c, col0:col0 + 128], ps[:, b, cc * 128:(cc + 1) * 128], gscale[:, c:c + 1], None, ALU.mult),
                    reads=[*bank[b], R["gsc"]], writes=[R[f"hT{col0 // 128}"]])

    MM_PIECES = [2]

    def mm8(out_ap, lhs_fn, rhs_fn, reads, b):
        npc = MM_PIECES[0]
        per = 8 // npc
        for hf in range(npc):
            def f(hf=hf):
                first = ins = None
                for c in range(per * hf, per * hf + per):
                    ins = nc.tensor.matmul(out_ap, lhs_fn(c), rhs_fn(c), start=(c == 0), stop=(c == 7))
                    first = first or ins
                return first, ins
            S.op("pe", f, reads=reads, writes=bank[b])

    R["gsc"].w = None
    def wait_params():
        S.op("dve", lambda: nc.vector.tensor_copy(stat[:, 60:61], g_fm[:, 0:1]),
             reads=[R["g_fm"], R["mg_fm"], R["ps_fm"], R["qg_b"], R["kg_b"], R["invw"], R["corr"], R["pw"]],
             writes=[R["gsc"]])
    wait_params()

    BGB = [5, 6, 7]
    bgi = [0]

    def nb():
        b = BGB[bgi[0] % len(BGB)]
        bgi[0] += 1
        return b

    def MS():
        def mtile(s):
            xt = xtb[s]
            xkey = f"xt{s}"
            S.dma("sp", xt[:], mem_d[s * 128:(s + 1) * 128, :], ds[xkey], writes=[R[xkey]])
            norm_and_transpose(xt, xkey, stat[:, s:s + 1], mg_fm, s * 128, [nb(), nb()])
        mtile(0)
        mtile(1)
        tb = nb()

        def mkv(s):
            b = nb()
            if b == tb:
                b = nb()
            mm8(ps[:, b, :], lambda c: hT[:, c, s * 128:(s + 1) * 128], lambda c: wkv[:, c, 0:512], [R[f"hT{s}"], R["V3"]], b)
            S.op("dve", lambda: nc.vector.tensor_copy(MV[:, s, :], ps[:, b, 256:512]), reads=bank[b], writes=[R["MV"]])
            S.op("dve", lambda: nc.vector.tensor_copy(kf[:], ps[:, b, 0:256]), reads=bank[b], writes=[R["mq0"]])
            S.op("dve", lambda: nc.vector.tensor_tensor(msq[:], kf[:], kf[:], ALU.mult), reads=[R["mq0"]], writes=[R["g2_1"]])
            kss = stat[:, 8 + 4 * s:12 + 4 * s]
            S.op("dve", lambda: nc.vector.tensor_reduce(kss, msq[:].rearrange("p (h d) -> p h d", d=64), AX.X, ALU.add),
                 reads=[R["g2_1"]], writes=[R["kss"]])
            S.op("act", lambda: nc.scalar.activation(kss, kss, AF.Ln, bias=EPS, scale=1.0 / 64), reads=[R["kss"]], writes=[R["kss"]])
            S.op("act", lambda: nc.scalar.activation(kss, kss, AF.Exp, scale=-0.5), reads=[R["kss"]], writes=[R["kss"]])

            def hn(h):
                S.op("dve", lambda: nc.vector.scalar_tensor_tensor(
                    kf[:, h * 64:(h + 1) * 64], kf[:, h * 64:(h + 1) * 64], kss[:, h:h + 1], kg_b[:], ALU.mult, ALU.mult),
                    reads=[R["mq0"], R["kss"], R["gsc"]], writes=[R["mq0"]])
            for h in range(4):
                hn(h)

            def trk():
                first = ins = None
                for pr in range(2):
                    ins = nc.tensor.transpose(ps[:, tb, pr * 256 + s * 128: pr * 256 + (s + 1) * 128],
                                              kf[:, pr * 128:(pr + 1) * 128], ident_f[:])
                    first = first or ins
                return first, ins
            S.op("pe", trk, reads=[R["mq0"], R["ident_f"]], writes=bank[tb])
        mkv(0)
        mkv(1)

        def kn(pr):
            S.op("dve", lambda: nc.vector.tensor_scalar(KNT[:, pr, :], ps[:, tb, pr * 256:(pr + 1) * 256], 0.125, None, ALU.mult),
                 reads=bank[tb], writes=[R["KNT"]])
        kn(0)
        kn(1)
        S.op("dve", lambda: nc.vector.memset(Vz[:, 12:16], 0.0), reads=[R["V3"]], writes=[R["V3"]])


    def PA(tt, with_q=True, part="all"):
        t0 = tt * 512
        qb = 0
        core = part in ("all", "core")
        rest = part in ("all", "rest")

        def xtile(s):
            xt = xtb[s % 2]
            xkey = f"xt{s % 2}"
            S.dma("sp", xt[:], x_d[t0 + s * 128: t0 + (s + 1) * 128, :], ds[xkey], writes=[R[xkey]])
            norm_and_transpose(xt, xkey, stat[:, 16 + s:17 + s], g_fm, s * 128, [nb(), nb()])
        if core:
            for s in range(4):
                xtile(s)

        def kproj(j):
            b = nb()
            col = 1024 + j * 128
            mm8(ps[:, b, :], lambda c: win[:, c, col:col + 128], lambda c: hT[:, c, :], [R["hT0"], R["hT1"], R["hT2"], R["hT3"], winres(col)], b)
            S.op("dve", lambda: nc.vector.tensor_scalar(KT[:, j, t0:t0 + 512], ps[:, b, :], 0.125, None, ALU.mult),
                 reads=bank[b], writes=[R[f"KT{tt}"]])

        def mqproj(s):
            b = nb()
            m = mq[s % 2]
            mk = f"mq{s % 2}"
            mm8(ps[:, b, 0:256], lambda c: hT[:, c, s * 128:(s + 1) * 128], lambda c: win[:, c, 2560:2816], [R[f"hT{s}"], winres(2560)], b)
            S.op("dve", lambda: nc.vector.tensor_copy(m[:], ps[:, b, 0:256]), reads=bank[b], writes=[R[mk]])
            S.op("dve", lambda: nc.vector.tensor_tensor(msq[:], m[:], m[:], ALU.mult), reads=[R[mk]], writes=[R["g2_1"]])
            qss = stat[:, 24 + 4 * s:28 + 4 * s]
            S.op("dve", lambda: nc.vector.tensor_reduce(qss, msq[:].rearrange("p (h d) -> p h d", d=64), AX.X, ALU.add),
                 reads=[R["g2_1"]], writes=[R["qss"]])
            S.op("act", lambda: nc.scalar.activation(qss, qss, AF.Ln, bias=EPS, scale=1.0 / 64), reads=[R["qss"]], writes=[R["qss"]])
            S.op("act", lambda: nc.scalar.activation(qss, qss, AF.Exp, scale=-0.5), reads=[R["qss"]], writes=[R["qss"]])

            def hn(h):
                S.op("dve", lambda: nc.vector.scalar_tensor_tensor(
                    m[:, h * 64:(h + 1) * 64], m[:, h * 64:(h + 1) * 64], qss[:, h:h + 1], qg_b[:], ALU.mult, ALU.mult),
                    reads=[R[mk], R["qss"], R["gsc"]], writes=[R[mk]])
            for h in range(4):
                hn(h)

        def trq(s):
            m = mq[s % 2]
            mk = f"mq{s % 2}"
            b = nb()

            def f():
                first = ins = None
                for pr in range(2):
                    ins = nc.tensor.transpose(ps[:, b, pr * 128:(pr + 1) * 128], m[:, pr * 128:(pr + 1) * 128], ident_f[:])
                    first = first or ins
                return first, ins
            S.op("pe", f, reads=[R[mk], R["ident_f"]], writes=bank[b])
            S.op("dve", lambda: nc.vector.tensor_copy(QNT[:, :, s * 128:(s + 1) * 128],
                                                      ps[:, b, 0:256].rearrange("p (r t) -> p r t", r=2)),
                 reads=bank[b], writes=[R["QNT"]])

        if core:
            for j in range(4):
                kproj(j)

        def vproj(s):
            b = nb()
            mm8(ps[:, b, :], lambda c: hT[:, c, s * 128:(s + 1) * 128], lambda c: win[:, c, 1536:2048], [R[f"hT{s}"], winres(1536)], b)

            def ev(hh):
                S.op("dve", lambda: nc.vector.tensor_copy(
                    Vz[:, tt * 4 + s, :, hh, hh * 64:(hh + 1) * 64],
                    ps[:, b, :].rearrange("p (j h d) -> p j h d", h=2, d=64)[:, :, hh, :]), reads=bank[b], writes=[R[f"V{tt}"]])
            ev(0)
            ev(1)
        if core:
            for s in range(4):
                vproj(s)

        def qproj(j):
            b = nb()
            col = 512 + j * 128
            mm8(ps[:, b, :], lambda c: win[:, c, col:col + 128], lambda c: hT[:, c, :], [R["hT0"], R["hT1"], R["hT2"], R["hT3"], winres(col)], b)

            def ev(hh):
                S.op("dve", lambda: nc.vector.tensor_copy(QTz[hh * 64:(hh + 1) * 64, qb, 2 * j + hh, :], ps[hh * 64:(hh + 1) * 64, b, :]),
                     reads=bank[b], writes=[R[f"QT{qb}"]])
            ev(0)
            ev(1)
        if with_q and core:
            for j in range(4):
                qproj(j)

        def uproj(j):
            if tt == 0:
                S.op("dve", lambda: nc.vector.memset(ubuf[:, j, 0:16], 0.0), writes=[R["ubuf"]])
            else:
                S.op("dve", lambda: nc.vector.tensor_copy(ubuf[:, j, 0:16], ubuf[:, j, 512:528]), reads=[R["ubuf"]], writes=[R["ubuf"]])
            b = nb()
            col = j * 128
            mm8(ps[:, b, :], lambda c: win[:, c, col:col + 128], lambda c: hT[:, c, :], [R["hT0"], R["hT1"], R["hT2"], R["hT3"], winres(col)], b)
            S.op("dve", lambda: nc.vector.tensor_copy(ubuf[:, j, 16:528], ps[:, b, :]), reads=bank[b], writes=[R["ubuf"]])
        if rest:
            mqproj(0); mqproj(1); trq(0); mqproj(2); trq(1); mqproj(3); trq(2)
            for j in range(2):
                uproj(j)
            trq(3)

    def QP(tt):
        def qproj(j):
            b = nb()
            col = 512 + j * 128
            mm8(ps[:, b, :], lambda c: win[:, c, col:col + 128], lambda c: hT[:, c, :], [R["hT0"], R["hT1"], R["hT2"], R["hT3"], winres(col)], b)

            def ev(hh):
                S.op("dve", lambda: nc.vector.tensor_copy(QTz[hh * 64:(hh + 1) * 64, 0, 2 * j + hh, :], ps[hh * 64:(hh + 1) * 64, b, :]),
                     reads=bank[b], writes=[R["QT0"]])
            ev(0)
            ev(1)
        for j in range(4):
            qproj(j)

    def G(tt):
        order = [(2, 2048), (3, 2176), (4, 2304), (5, 2432), (0, 256), (1, 384), (6, 2816), (7, 2944)]

        def one(gi, mc, gcol):
            gb = nb()
            mm8(ps[:, gb, :], lambda c: win[:, c, gcol:gcol + 128], lambda c: hT[:, c, :], [R["hT0"], R["hT1"], R["hT2"], R["hT3"], winres(gcol)], gb)
            t1, t2 = g1[gi % 2], g2[gi % 2]
            k1, k2 = f"g1_{gi % 2}", f"g2_{gi % 2}"
            S.op("act", lambda: nc.scalar.activation(t1[:], ps[:, gb, :], AF.Exp, scale=-1.0), reads=bank[gb], writes=[R[k1]])
            S.op("act", lambda: nc.scalar.activation(t1[:], t1[:], AF.Ln, bias=1.0), reads=[R[k1]], writes=[R[k1]])
            S.op("act", lambda: nc.scalar.activation(t2[:], t1[:], AF.Exp, scale=-1.0), reads=[R[k1]], writes=[R[k2]])
            S.op("dve", lambda: nc.vector.tensor_tensor(sg[:, mc, :], ps[:, gb, :], t2[:], ALU.mult),
                 reads=[*bank[gb], R[k2]], writes=[R[f"sg{mc}"]])
        for gi, (mc, gcol) in enumerate(order):
            one(gi, mc, gcol)

    def PM(tt):
        def pool_chunk(j):
            S.op("dve", lambda: nc.vector.tensor_tensor(pa[:, 1:528], ubuf[:, j, 1:528], ubuf[:, j, 0:527], ALU.add),
                 reads=[R["ubuf"]], writes=[R["pa"]])
            S.op("dve", lambda: nc.vector.tensor_tensor(pb[:, 3:528], pa[:, 3:528], pa[:, 1:526], ALU.add),
                 reads=[R["pa"]], writes=[R["pb"]])
            if j == 1:
                S.op("dve", lambda: nc.vector.tensor_tensor(pa[:, 7:528], pb[:, 7:528], pb[:, 3:524], ALU.add),
                     reads=[R["pb"]], writes=[R["pa"]])
                S.op("dve", lambda: nc.vector.tensor_tensor(pb[64:128, 15:528], pa[64:128, 15:528], pa[64:128, 7:520], ALU.add),
                     reads=[R["pa"]], writes=[R["pb"]])
            if tt == 0:
                S.op("dve", lambda: nc.vector.tensor_tensor(pa[0:64, 16:32], pa[0:64, 16:32], corr[0:64, j, :], ALU.mult),
                     reads=[R["pa"], R["gsc"]], writes=[R["pa"]])
                S.op("dve", lambda: nc.vector.tensor_tensor(pb[64:128, 16:32], pb[64:128, 16:32], corr[64:128, j, :], ALU.mult),
                     reads=[R["pb"], R["gsc"]], writes=[R["pb"]])
            S.op("dve", lambda: nc.vector.scalar_tensor_tensor(pooled[0:64, j, :], pa[0:64, 16:528], invw[0:64, j:j + 1],
                                                               ubuf[0:64, j, 16:528], ALU.mult, ALU.subtract),
                 reads=[R["pa"], R["ubuf"], R["gsc"]], writes=[R["pooled"]])
            S.op("dve", lambda: nc.vector.scalar_tensor_tensor(pooled[64:128, j, :], pb[64:128, 16:528], invw[64:128, j:j + 1],
                                                               ubuf[64:128, j, 16:528], ALU.mult, ALU.subtract),
                 reads=[R["pb"], R["ubuf"], R["gsc"]], writes=[R["pooled"]])
            yb = nb()

            def ymm():
                first = ins = None
                for half in range(2):
                    r0 = half * 64
                    ins = nc.tensor.matmul(ps[r0:r0 + 64, yb, :], pw[r0:r0 + 64, j, :], pooled[r0:r0 + 64, j, :], start=True, stop=True)
                    first = first or ins
                return first, ins
            S.op("pe", ymm, reads=[R["pooled"], R["gsc"]], writes=bank[yb])
            S.op("dve", lambda: nc.vector.scalar_tensor_tensor(mixed[:, j, :], ps[:, yb, :], ps_fm[:, j:j + 1], sg[:, j, :], ALU.mult, ALU.mult),
                 reads=[*bank[yb], R[f"sg{j}"], R["gsc"]], writes=[R["mixed"]])
        pool_chunk(0)
        pool_chunk(1)

        def mem_pair(pr):
            nbk, dbk, sbk = nb(), nb(), nb()

            def head(hh):
                hm = 2 * pr + hh
                r0 = hh * 64

                def chunk(s2_):
                    S.op("pe", lambda: nc.tensor.matmul(ps[:, sbk, :], KNT[r0:r0 + 64, pr, s2_ * 128:(s2_ + 1) * 128], QNT[r0:r0 + 64, pr, :],
                                                        start=True, stop=True),
                         reads=[R["KNT"], R["QNT"]], writes=bank[sbk])
                    pt = pT[0]
                    S.op("act", lambda: nc.scalar.activation(pt[:], ps[:, sbk, :], AF.Exp, bias=-8.0), reads=bank[sbk], writes=[R["pT0"]])
                    S.op("pe", lambda: nc.tensor.matmul(ps[r0:r0 + 64, nbk, :], MV[:, s2_, hm * 64:(hm + 1) * 64], pt[:],
                                                        start=(s2_ == 0), stop=(s2_ == 1)),
                         reads=[R["pT0"], R["MV"]], writes=[bank[nbk][hh]])
                    S.op("pe", lambda: nc.tensor.matmul(ps[r0:r0 + 64, dbk, :], ones_bf[:], pt[:], start=(s2_ == 0), stop=(s2_ == 1)),
                         reads=[R["pT0"], R["ones_bf"]], writes=[bank[dbk][hh]])
                chunk(0)
                chunk(1)
            head(0)
            head(1)
            rd = g2[0]
            S.op("act", lambda: nc.scalar.activation(rd[:], ps[:, dbk, :], AF.Ln), reads=bank[dbk], writes=[R["g2_0"]])
            S.op("act", lambda: nc.scalar.activation(rd[:], rd[:], AF.Exp, scale=-1.0), reads=[R["g2_0"]], writes=[R["g2_0"]])
            S.op("dve", lambda: nc.vector.tensor_tensor(rd[:], ps[:, nbk, :], rd[:], ALU.mult), reads=[*bank[nbk], R["g2_0"]], writes=[R["g2_0"]])
            S.op("dve", lambda: nc.vector.tensor_tensor(mixed[:, 6 + pr, :], rd[:], sg[:, 6 + pr, :], ALU.mult),
                 reads=[R["g2_0"], R[f"sg{6 + pr}"]], writes=[R["mixed"]])
        mem_pair(0)
        mem_pair(1)

    def OP(tt):
        t0 = tt * 512

        def sub(s):
            xr = xtb[s % 2]
            xk = f"xt{s % 2}"
            S.dma("sp", xr[:], x_d[t0 + s * 128: t0 + (s + 1) * 128, :], ds[xk + "_ld"], writes=[R[xk]])

            def half(nh):
                b = nb()
                mm8(ps[:, b, :], lambda c: mixed[:, c, s * 128:(s + 1) * 128], lambda c: wout[:, c, nh * 512:(nh + 1) * 512],
                    [R["mixed"], R["wout"]], b)
                S.op("dve", lambda: nc.vector.tensor_tensor(xr[:, nh * 512:(nh + 1) * 512], ps[:, b, :], xr[:, nh * 512:(nh + 1) * 512], ALU.add),
                     reads=[*bank[b], R[xk]], writes=[R[xk]])
            half(0)
            half(1)
            S.dma("pool", out_d[t0 + s * 128: t0 + (s + 1) * 128, :], xr[:], ds[xk + "_st"], reads=[R[xk]])
        for s in range(4):
            sub(s)

    def AT(tt, bg, bg_tail):
        nkb = 4 * tt + 4
        qb = 0
        items = [(2 * pair + hh, kb) for pair in range(4) for kb in reversed(range(nkb)) for hh in range(2)]
        n_it = len(items)

        def geom(n):
            h, kb = items[n]
            jd = kb - 4 * tt
            c0 = max(jd, 0) * 128
            return h, kb, h // 2, c0, (jd >= 0)

        def s1a(n):
            h, kb, j, c0, diag = geom(n)
            zb = n % 2

            def f():
                return nc.tensor.matmul(ps[:, zb, c0:512], KT[:, j, kb * 128:(kb + 1) * 128],
                                        QTz[:, qb, h, c0:512], start=True, stop=True)
            S.op("pe", f, reads=[R[f"KT{kb // 4}"], R[f"QT{qb}"]], writes=bank[zb])
            S.op("act", lambda: nc.scalar.activation(e_t[n % NE][:, c0:512], ps[:, zb, c0:512], AF.Exp),
                 reads=bank[zb], writes=[R[f"e{n % NE}"]])

        def s1m(n):
            h, kb, j, c0, diag = geom(n)
            if diag:
                S.op("dve", lambda: nc.vector.tensor_tensor(e_t[n % NE][:, c0:c0 + 128], e_t[n % NE][:, c0:c0 + 128], m01[:], ALU.mult),
                     reads=[R[f"e{n % NE}"], R["m01"]], writes=[R[f"e{n % NE}"]])

        def s1b(n):
            h, kb, j, c0, diag = geom(n)
            S.op("act", lambda: nc.scalar.activation(sp_t[n % NE][:, c0:512], e_t[n % NE][:, c0:512], AF.Ln, bias=1.0),
                 reads=[R[f"e{n % NE}"]], writes=[R[f"sp{n % NE}"]])

        def s2nu(n):
            h, kb, j, c0, diag = geom(n)
            ab = 2 + h % 2
            S.op("pe", lambda: nc.tensor.matmul(ps[:, ab, c0:512], nu_r[:], sp_t[n % NE][:, c0:512], start=(kb == nkb - 1), stop=(kb == 0),
                                                skip_group_check=True),
                 reads=[R[f"sp{n % NE}"], R["nu_r"]], writes=bank[ab])

        def s2p(n):
            h, kb, j, c0, diag = geom(n)
            ab = 2 + h % 2
            S.op("act", lambda: nc.scalar.activation(P_t[n % 2][:, c0:512], ps[:, ab, c0:512], AF.Exp),
                 reads=bank[ab], writes=[R[f"P{n % 2}"]])

        def s2b(n):
            h, kb, j, c0, diag = geom(n)
            ab = 2 + h % 2
            S.op("dve", lambda: nc.vector.tensor_tensor(a_t[n % 2][:, c0:512], e_t[n % NE][:, c0:512], P_t[n % 2][:, c0:512], ALU.mult),
                 reads=[R[f"e{n % NE}"], R[f"P{n % 2}"]], writes=[R[f"a{n % 2}"]])
            if kb != 0:
                S.op("pe", lambda: nc.tensor.matmul(ps[:, ab, c0:512], nl_r[:], sp_t[n % NE][:, c0:512], start=False, stop=False,
                                                    skip_group_check=True),
                     reads=[R[f"sp{n % NE}"], R["nl_r"]], writes=bank[ab])

        def s2c(n):
            h, kb, j, c0, diag = geom(n)
            first = (kb == nkb - 1 and h % 2 == 0)
            last = (kb == 0 and h % 2 == 1)
            S.op("pe", lambda: nc.tensor.matmul(ps[:, 4, c0:512], Vz[:, kb, j, h % 2, :], a_t[n % 2][:, c0:512],
                                                start=first, stop=last, skip_group_check=True),
                 reads=[R[f"a{n % 2}"], R[f"V{kb // 4}"]], writes=bank[4])
            if last:
                need_bg(lambda ent: any(r is R[f"sg{2 + j}"] for r in ent[3]) or any(r is R["mixed"] for r in ent[2]))
                S.op("dve", lambda: nc.vector.tensor_tensor(mixed[:, 2 + j, :], ps[:, 4, :], sg[:, 2 + j, :], ALU.mult),
                     reads=[*bank[4], R[f"sg{2 + j}"]], writes=[R["mixed"]])

        def ok(n):
            return 0 <= n < n_it
        pos = 0
        tpos = 0

        def need_bg(pred):
            nonlocal pos
            idx = None
            for k in range(pos, len(bg)):
                if pred(bg[k]):
                    idx = k
            if idx is not None:
                while pos <= idx:
                    S.play(bg[pos])
                    pos += 1

        def blocked(ent, fresh):
            eng = ent[1][0]
            hit = [fresh[id(r)] for r in ent[2] + ent[3] if id(r) in fresh]
            if not hit:
                return False
            if eng == "act":
                return tt == NT - 1 and any(h != "act" for h in hit)
            if eng == "pe":
                return any(h != "pe" for h in hit)
            return False
        quota = (13 * len(bg)) // (10 * n_it) + 2
        for i in range(-6, n_it):
            if ok(i + 6):
                s1a(i + 6)
            if ok(i + 5):
                s1m(i + 5)
            if ok(i + 4):
                s1b(i + 4)
            if ok(i + 1):
                s2b(i + 1)
            if ok(i + 3):
                s2nu(i + 3)
            if ok(i + 2):
                s2p(i + 2)
            if ok(i):
                s2c(i)
            fresh = {}
            cnt = 0
            while pos < len(bg) and cnt < quota:
                ent = bg[pos]
                if blocked(ent, fresh):
                    break
                S.play(ent)
                for r in ent[3]:
                    fresh[id(r)] = ent[1][0]
                pos += 1
                cnt += 1
            while pos >= len(bg) and i + 6 >= n_it - 1 and tpos < len(bg_tail) and cnt < quota + 3:
                ent = bg_tail[tpos]
                if blocked(ent, fresh):
                    break
                S.play(ent)
                for r in ent[3]:
                    fresh[id(r)] = ent[1][0]
                tpos += 1
                cnt += 1
        while pos < len(bg):
            S.play(bg[pos])
            pos += 1
        while tpos < len(bg_tail):
            S.play(bg_tail[tpos])
            tpos += 1

    S.rec = []
    PA(0, part="core")
    fg0, S.rec = reorder(S.rec), None
    for ent in fg0:
        S.play(ent)
    for tt in range(NT):
        S.rec = []
        MM_PIECES[0] = 4 if tt == NT - 1 else 2
        if tt == 0:
            PA(0, part="rest")
        if tt > 0:
            OP(tt - 1)
        G(tt)
        if tt == 0:
            MS()
        PM(tt)
        if tt + 1 < NT:
            PA(tt + 1, with_q=False)
        bg, S.rec = reorder(S.rec), []
        if tt + 1 < NT:
            QP(tt + 1)
        bg_tail, S.rec = reorder(S.rec), None
        AT(tt, bg, bg_tail)
    MM_PIECES[0] = 2
    BGB[:] = [5, 6, 7, 0, 1, 2, 3, 4]
    S.rec = []
    OP(NT - 1)
    fin, S.rec = reorder(S.rec), None
    for ent in fin:
        S.play(ent)

    for i in range(2):
        d = ds[f"xt{i}_st"]
        nc.gpsimd.wait_ge(d.h, d.v)
    return nc


_CACHE = {}


def kernel(x, mem, norm_g, w_in, pool_w, pool_scale, mem_norm_g, w_mem_kv, q_norm_g, k_norm_g, w_out):
    B = x.shape[0]
    f = np.float32
    x = np.ascontiguousarray(x, dtype=f)
    mem = np.ascontiguousarray(mem, dtype=f)
    shared = host_constants()
    shared["w_in"] = np.ascontiguousarray(w_in[0], dtype=f)
    shared["w_mem_kv"] = np.ascontiguousarray(w_mem_kv[0], dtype=f)
    shared["w_out"] = np.ascontiguousarray(w_out[0], dtype=f)
    shared["g_fm"] = np.ascontiguousarray(np.asarray(norm_g[0], f).reshape(8, 128).T)
    shared["mg_fm"] = np.ascontiguousarray(np.asarray(mem_norm_g[0], f).reshape(8, 128).T)
    shared["ps_fm"] = np.ascontiguousarray(np.asarray(pool_scale[0], f).reshape(2, 128).T)
    shared["qg_b"] = np.ascontiguousarray(np.broadcast_to(np.asarray(q_norm_g[0], f)[None, :], (128, 64)))
    shared["kg_b"] = np.ascontiguousarray(np.broadcast_to(np.asarray(k_norm_g[0], f)[None, :], (128, 64)))
    shared["pw"] = np.ascontiguousarray(np.asarray(pool_w[0], f).reshape(2, 2, 64, 64).transpose(1, 2, 0, 3).reshape(128, 2, 64))
    if "nc" not in _CACHE:
        _CACHE["nc"] = build_nc()
    nc = _CACHE["nc"]
    in_maps = []
    for b in range(B):
        m = dict(shared)
        m["x"] = x[b]
        m["mem"] = mem[b]
        in_maps.append(m)
    res = run_bass_kernel_spmd(nc, in_maps, core_ids=list(range(B)))
    return np.stack([np.asarray(r["out"], dtype=f) for r in res.results], axis=0)
```
